# Optimizing a Trainium2 kernel written in Bass

```python
import jax, jax.numpy as jnp
from jax import lax
import numpy as np

D_MODEL = 1024
BATCH = 4
SEQ = 8192
DEPTH = 2

N_META = 16
POOL_WINDOWS = (2, 4, 8, 16)
N_POOL_GROUPS = len(POOL_WINDOWS)
POOL_GROUP_DIM = D_MODEL // N_POOL_GROUPS
HEAD_DIM = 64
N_HEADS = D_MODEL // HEAD_DIM
D_FF = ((8 * D_MODEL // 3 + 127) // 128) * 128
CONV_WIDTH = 3
Q_BLOCK = 128
N_A_LAYERS = DEPTH // 2
N_B_LAYERS = DEPTH - N_A_LAYERS
ALPHA = (2.0 * DEPTH) ** 0.25
BETA = (8.0 * DEPTH) ** -0.25
LN_EPS = 1e-5
NEG_INF = -1e30

kernel_name = "yoco_pool_fox_convffn_deepnorm_meta"


def layer_norm(x, g, b):
    xf = x.astype(jnp.float32)
    mu = jnp.mean(xf, axis=-1, keepdims=True)
    xc = xf - mu
    var = jnp.mean(xc * xc, axis=-1, keepdims=True)
    y = xc * lax.rsqrt(var + LN_EPS) * g.astype(jnp.float32) + b.astype(jnp.float32)
    return y.astype(x.dtype)


def multiscale_pool_mixer(h, w_group, scale):
    b_, L, D = h.shape
    G = POOL_GROUP_DIM
    hf = h.astype(jnp.float32)
    cs0 = jnp.pad(jnp.cumsum(hf, axis=1), ((0, 0), (1, 0), (0, 0)))
    t = jnp.arange(1, L + 1, dtype=jnp.float32)
    outs = []
    for g, w in enumerate(POOL_WINDOWS):
        sl = slice(g * G, (g + 1) * G)
        upper = cs0[:, 1:, sl]
        lower = jnp.pad(cs0[:, :L + 1 - w, sl], ((0, 0), (w - 1, 0), (0, 0)))
        count = jnp.minimum(t, float(w))[None, :, None]
        outs.append((upper - lower) / count)
    pooled = jnp.concatenate(outs, axis=-1)
    diff = (pooled - hf).astype(h.dtype).reshape(b_, L, N_POOL_GROUPS, G)
    mixed = jnp.einsum('blgc,gcd->blgd', diff, w_group).reshape(b_, L, D)
    return mixed * scale


def conv_glu_ffn(h, w_in, conv_w, conv_b, w_out):
    L = h.shape[1]
    u = h @ w_in
    up = jnp.pad(u, ((0, 0), (CONV_WIDTH - 1, 0), (0, 0)))
    c = conv_b + sum(conv_w[k] * up[:, k:k + L] for k in range(CONV_WIDTH))
    a, g = jnp.split(c, 2, axis=-1)
    return (jax.nn.silu(a) * g) @ w_out


def padded_layout(L):
    front = (-N_META) % Q_BLOCK
    total = ((front + L + Q_BLOCK - 1) // Q_BLOCK) * Q_BLOCK
    return front, total


def shared_kv(h, w_kv, w_f, b_f):
    b_, L, D = h.shape
    front, Lp = padded_layout(L)
    pad = ((0, 0), (front, Lp - front - L), (0, 0), (0, 0))
    kv = h @ w_kv
    k = jnp.pad(kv[..., :D].reshape(b_, L, N_HEADS, HEAD_DIM), pad).transpose(0, 2, 1, 3)
    v = jnp.pad(kv[..., D:].reshape(b_, L, N_HEADS, HEAD_DIM), pad).transpose(0, 2, 1, 3)
    logf = jax.nn.log_sigmoid((h @ w_f).astype(jnp.float32) + b_f.astype(jnp.float32))
    logf = jnp.pad(logf, ((0, 0), (front, Lp - front - L), (0, 0)))
    c = jnp.cumsum(logf, axis=1).transpose(0, 2, 1)
    return k, v, c


def forgetting_attention(h, w_q, w_o, k, v, c):
    b_, L, D = h.shape
    front, Lp = padded_layout(L)
    nb = Lp // Q_BLOCK
    q = (h @ w_q).reshape(b_, L, N_HEADS, HEAD_DIM)
    q = jnp.pad(q, ((0, 0), (front, Lp - front - L), (0, 0), (0, 0)))
    qb = q.reshape(b_, nb, Q_BLOCK, N_HEADS, HEAD_DIM).transpose(1, 0, 3, 2, 4)
    cq = c.reshape(b_, N_HEADS, nb, Q_BLOCK).transpose(2, 0, 1, 3)
    kpos = jnp.arange(Lp)
    scale = HEAD_DIM ** -0.5

    def block(args):
        i, q_i, cq_i = args
        qpos = i * Q_BLOCK + jnp.arange(Q_BLOCK)
        s = jnp.einsum('bhqd,bhkd->bhqk', q_i, k, preferred_element_type=jnp.float32) * scale
        s = s + cq_i[..., None] - c[:, :, None, :]
        mask = (kpos[None, :] <= qpos[:, None]) & (kpos[None, :] >= front)
        p = jax.nn.softmax(jnp.where(mask, s, NEG_INF), axis=-1)
        return jnp.einsum('bhqk,bhkd->bhqd', p.astype(v.dtype), v)

    o = lax.map(block, (jnp.arange(nb), qb, cq))
    o = o.transpose(1, 0, 3, 2, 4).reshape(b_, Lp, D)[:, front:front + L]
    return o @ w_o


def setup_inputs(seed: int = 0) -> dict:
    key = jax.random.key(seed)
    ks = jax.random.split(key, 16)
    D, F, G, H = D_MODEL, D_FF, POOL_GROUP_DIM, N_HEADS
    nrm = jax.random.normal
    return {
        "x": nrm(ks[0], (BATCH, SEQ, D), jnp.float32),
        "meta": nrm(ks[1], (N_META, D), jnp.float32),
        "pool_w": nrm(ks[2], (N_A_LAYERS, N_POOL_GROUPS, G, G), jnp.float32) * (G ** -0.5) * BETA,
        "pool_scale": 1.0 + 0.02 * nrm(ks[3], (N_A_LAYERS, D), jnp.float32),
        "w_kv": nrm(ks[4], (D, 2 * D), jnp.float32) * (D ** -0.5),
        "w_f": nrm(ks[5], (D, H), jnp.float32) * (D ** -0.5),
        "b_f": jax.random.uniform(ks[6], (H,), jnp.float32, 1.0, 6.0),
        "w_q": nrm(ks[7], (N_B_LAYERS, D, D), jnp.float32) * (D ** -0.5),
        "w_o": nrm(ks[8], (N_B_LAYERS, D, D), jnp.float32) * (D ** -0.5) * BETA,
        "ffn_w_in": nrm(ks[9], (DEPTH, D, 2 * F), jnp.float32) * (D ** -0.5),
        "ffn_conv_w": nrm(ks[10], (DEPTH, CONV_WIDTH, 2 * F), jnp.float32) * (CONV_WIDTH ** -0.5),
        "ffn_conv_b": 0.02 * nrm(ks[11], (DEPTH, 2 * F), jnp.float32),
        "ffn_w_out": nrm(ks[12], (DEPTH, F, D), jnp.float32) * (F ** -0.5) * BETA,
        "ln_g": 1.0 + 0.02 * nrm(ks[13], (DEPTH, 2, D), jnp.float32),
        "ln_b": 0.02 * nrm(ks[14], (DEPTH, 2, D), jnp.float32),
    }


def reference(x, meta, pool_w, pool_scale, w_kv, w_f, b_f, w_q, w_o, ffn_w_in, ffn_conv_w,
              ffn_conv_b, ffn_w_out, ln_g, ln_b):
    b_ = x.shape[0]
    h = jnp.concatenate(
        [jnp.broadcast_to(meta[None].astype(x.dtype), (b_, N_META, D_MODEL)), x], axis=1)
    shared = None
    for i in range(DEPTH):
        if i < N_A_LAYERS:
            mix = multiscale_pool_mixer(h, pool_w[i], pool_scale[i])
        else:
            if i == N_A_LAYERS:
                shared = shared_kv(h, w_kv, w_f, b_f)
            j = i - N_A_LAYERS
            mix = forgetting_attention(h, w_q[j], w_o[j], shared[0], shared[1], shared[2])
        h = layer_norm(ALPHA * h + mix, ln_g[i, 0], ln_b[i, 0])
        ffn = conv_glu_ffn(h, ffn_w_in[i], ffn_conv_w[i], ffn_conv_b[i], ffn_w_out[i])
        h = layer_norm(ALPHA * h + ffn, ln_g[i, 1], ln_b[i, 1])
    return h[:, N_META:]
```

```python
import numpy as np
from contextlib import ExitStack
import ml_dtypes
import concourse.bass as bass
import concourse.mybir as mybir
from concourse.bass_utils import run_bass_kernel_spmd

F32 = mybir.dt.float32
BF16 = mybir.dt.bfloat16
AF = mybir.ActivationFunctionType
ALU = mybir.AluOpType
NPBF = ml_dtypes.bfloat16

D = 1024
NCT = 8
F = 2816
F2 = 5632
NFT = 44
NPAIR = 22
H = 16
DH = 64
NMETA = 16
SEQ = 8192
T = NMETA + SEQ
BATCH = 4
ALPHA = float(4.0 ** 0.25)
EPS = 1e-5
TS = 256
NT = 16
NTOK = NMETA + NT * TS
PRE = 112
NR = PRE + NTOK
NEG = -30000.0
NH = 8
QT_ = 512
ARENA_KB = 204

ENGS = ("pe", "act", "dve", "pool", "sp")


class Buf:
    __slots__ = ("name", "w", "r", "rd")

    def __init__(self, name=""):
        self.name = name
        self.w = None
        self.r = {}
        self.rd = []


class Ev:
    __slots__ = ("eng", "fn", "deps", "need_inc", "val", "dma", "key")

    def __init__(self, eng, fn, dma, key):
        self.eng = eng
        self.fn = fn
        self.deps = []
        self.need_inc = False
        self.val = None
        self.dma = dma
        self.key = key


class Sched:
    def __init__(self, nc):
        self.nc = nc
        self.ops = {e: [] for e in ENGS}
        self.dma_counts = {}
        self.same_engine_sync = {"act", "dve", "pool"}
        self.all_dma = []

    def op(self, eng, fn, reads=(), writes=(), dma=None):
        ev = Ev(eng, fn, dma is not None, dma)
        deps = []
        for b in reads:
            if b.w is not None:
                deps.append(b.w)
        for b in writes:
            if b.w is not None:
                deps.append(b.w)
            deps.extend(b.r.values())
            deps.extend(b.rd)
        seen = set()
        for d in deps:
            if id(d) in seen or d is ev:
                continue
            seen.add(id(d))
            if (not d.dma) and (not ev.dma) and d.eng == eng and eng not in self.same_engine_sync:
                continue
            ev.deps.append(d)
            if not d.dma:
                d.need_inc = True
        for b in reads:
            if ev.dma:
                b.rd.append(ev)
            else:
                b.r[eng] = ev
        for b in writes:
            b.w = ev
            b.r = {}
            b.rd = []
        if ev.dma:
            c = self.dma_counts.get(dma, 0) + 1
            self.dma_counts[dma] = c
            ev.val = 16 * c
            self.all_dma.append(ev)
        self.ops[eng].append(ev)
        return ev

    def barrier(self, eng, evs):
        ev = Ev(eng, None, False, None)
        for d in evs:
            ev.deps.append(d)
            if not d.dma:
                d.need_inc = True
        self.ops[eng].append(ev)
        return ev

    def full_barrier(self):
        last = []
        for e in ENGS:
            for ev in reversed(self.ops[e]):
                if ev.fn is not None and not ev.dma:
                    last.append(ev)
                    break
        lastd = {}
        for ev in self.all_dma:
            lastd[ev.key] = ev
        evs = last + list(lastd.values())
        for e in ENGS:
            self.barrier(e, evs)

    def emit(self, stack):
        nc = self.nc
        for e in ENGS:
            n = 0
            for ev in self.ops[e]:
                if not ev.dma and ev.need_inc:
                    n += 1
                    ev.val = n
        sem_eng = {e: stack.enter_context(nc.semaphore("s_" + e)) for e in ENGS}
        sem_dma = {k: stack.enter_context(nc.semaphore("d_" + k)) for k in self.dma_counts}
        block = stack.enter_context(nc.Block())

        def run(e, h):
            waited = {}
            for ev in self.ops[e]:
                for d in ev.deps:
                    sem = sem_dma[d.key] if d.dma else sem_eng[d.eng]
                    sid = id(sem)
                    if waited.get(sid, 0) < d.val:
                        h.wait_ge(sem, d.val)
                        waited[sid] = d.val
                if ev.fn is None:
                    continue
                inst = ev.fn(h)
                if ev.dma:
                    inst.then_inc(sem_dma[ev.key], 16)
                elif ev.need_inc:
                    inst.then_inc(sem_eng[e], 1)

        block.sync(lambda h: run("sp", h))
        block.tensor(lambda h: run("pe", h))
        block.scalar(lambda h: run("act", h))
        block.vector(lambda h: run("dve", h))
        block.gpsimd(lambda h: run("pool", h))


class Tl:
    __slots__ = ("ap", "b")

    def __init__(self, ap, name=""):
        self.ap = ap
        self.b = Buf(name)


class Cx:
    def __init__(self, nc, st):
        self.nc = nc
        self.st = st
        self.S = Sched(nc)
        self.n4 = ARENA_KB * 256
        self.arena = st.enter_context(nc.sbuf_tensor("arena", [128, self.n4], F32))
        self.off = 0
        self.ps = [st.enter_context(nc.psum_tensor(f"ps{i}", [128, 512], F32)) for i in range(8)]
        self.Bps = [Buf(f"ps{i}") for i in range(8)]
        self.pst = [self.ps[6][:, :].bitcast(BF16), self.ps[7][:, :].bitcast(BF16)]
        self.Bpst = [self.Bps[6], self.Bps[7]]
        self.out_evs = []

    def f32(self, n, name=""):
        o = self.off
        self.off += n
        assert self.off <= self.n4, ("arena overflow", name, self.off * 4 / 1024)
        return Tl(self.arena[:, o:o + n], name)

    def bf16(self, n, name=""):
        n4 = (n + 1) // 2
        o = self.off
        self.off += n4
        assert self.off <= self.n4, ("arena overflow", name, self.off * 4 / 1024)
        return Tl(self.arena[:, o:o + n4].bitcast(BF16), name)

    def mark(self):
        return self.off

    def release(self, m):
        self.off = m

    def dma(self, eng, out, in_, reads, writes, key):
        return self.S.op(eng, lambda h, o=out, i=in_: h.dma_start(out=o, in_=i), reads, writes, dma=key)

    def mm(self, out, lhsT, rhs, start, stop, reads, writes, skip=False):
        if skip:
            fn = lambda h, o=out, l=lhsT, r=rhs, a=start, b=stop: h.matmul(o, l, r, start=a, stop=b, skip_group_check=True)
        else:
            fn = lambda h, o=out, l=lhsT, r=rhs, a=start, b=stop: h.matmul(o, l, r, start=a, stop=b)
        return self.S.op("pe", fn, reads, writes)

    def act(self, out, in_, func, reads, writes, bias=None, scale=None):
        kw = {}
        if bias is not None:
            kw["bias"] = bias
        if scale is not None:
            kw["scale"] = scale
        return self.S.op("act", lambda h, o=out, i=in_, f=func, kw=kw: h.activation(out=o, in_=i, func=f, **kw), reads, writes)

    def copy(self, eng, out, in_, reads, writes):
        return self.S.op(eng, lambda h, o=out, i=in_: h.tensor_copy(o, i), reads, writes)

    def ts(self, eng, out, in0, s1, s2, op0, op1, reads, writes):
        if op1 is None:
            fn = lambda h, o=out, i=in0, a=s1, p=op0: h.tensor_scalar(o, i, a, None, op0=p)
        else:
            fn = lambda h, o=out, i=in0, a=s1, b=s2, p=op0, q=op1: h.tensor_scalar(o, i, a, b, op0=p, op1=q)
        return self.S.op(eng, fn, reads, writes)

    def tt(self, eng, out, in0, in1, op, reads, writes):
        return self.S.op(eng, lambda h, o=out, a=in0, b=in1, p=op: h.tensor_tensor(o, a, b, op=p), reads, writes)

    def stt(self, eng, out, in0, scalar, in1, op0, op1, reads, writes):
        return self.S.op(eng, lambda h, o=out, a=in0, s=scalar, b=in1, p=op0, q=op1:
                         h.scalar_tensor_tensor(o, a, s, b, op0=p, op1=q), reads, writes)


def subtiles(n):
    return [(o, min(128, n - o)) for o in range(0, n, 128)]


def token_tiles():
    return [(0, NMETA)] + [(NMETA + TS * i, TS) for i in range(NT)]


def emit_ln(cx, hp, n, lng, lnb, sms):
    S = cx.S
    for si, (o, ns) in enumerate(subtiles(n)):
        x = hp.ap[0:ns, si * D:(si + 1) * D]
        sm = sms[si]
        st0 = sm.ap[0:ns, 0:6]
        st1 = sm.ap[0:ns, 6:12]
        mv = sm.ap[0:ns, 12:14]
        rs = sm.ap[0:ns, 14:15]
        nm = sm.ap[0:ns, 15:16]
        stt = sm.ap[0:ns, 0:12].rearrange("p (a b) -> p a b", a=2)
        S.op("dve", lambda h, a=st0, b=x: h.bn_stats(a, b[:, 0:512]), [hp.b], [sm.b])
        S.op("dve", lambda h, a=st1, b=x: h.bn_stats(a, b[:, 512:1024]), [hp.b, sm.b], [sm.b])
        S.op("dve", lambda h, a=mv, b=stt: h.bn_aggr(a, b), [sm.b], [sm.b])
        cx.act(rs, mv[:, 1:2], AF.Sqrt, [sm.b, cx.eps.b], [sm.b], bias=cx.eps.ap[0:ns, 0:1], scale=1.0)
        S.op("dve", lambda h, a=rs: h.reciprocal(a, a), [sm.b], [sm.b])
        cx.stt("dve", nm, mv[:, 0:1], -1.0, rs, ALU.mult, ALU.mult, [sm.b], [sm.b])
        cx.act(x, x, AF.Identity, [hp.b, sm.b], [hp.b], bias=nm, scale=rs)
        cx.tt("pool", x, x, lng.ap[0:ns, :], ALU.mult, [hp.b, lng.b], [hp.b])
        cx.tt("pool", x, x, lnb.ap[0:ns, :], ALU.add, [hp.b, lnb.b], [hp.b])


def emit_transposes(cx, h, n, hb, hT, hT_dst=None, key="hTst"):
    S = cx.S
    subs = subtiles(n)
    for si, (o, ns) in enumerate(subs):
        cx.copy("pool", hb.ap[0:ns, si * D:(si + 1) * D], h.ap[0:ns, si * D:(si + 1) * D], [h.b], [hb.b])
    for c in range(NCT):
        half = c % 2
        pb = cx.Bpst[half]
        for si, (o, ns) in enumerate(subs):
            outp = cx.pst[half][:, o: o + ns]
            inp = hb.ap[0:ns, si * D + c * 128: si * D + (c + 1) * 128]
            S.op("pe", lambda hd, a=outp, b=inp, i=cx.ident.ap[0:ns, 0:ns]: hd.transpose(a, b, i),
                 [hb.b, cx.ident.b], [pb])
        cx.copy("dve", hT.ap[:, c * TS: c * TS + n], cx.pst[half][:, 0:n], [pb], [hT.b])
    if hT_dst is not None:
        dst, r0 = hT_dst
        cx.dma("sp", dst.ap.rearrange("(c p) t -> p c t", p=128)[:, :, r0:r0 + n],
               hT.ap.rearrange("p (c t) -> p c t", c=NCT)[:, :, 0:n], [hT.b], [dst.b], key)


def emit_ffn_pass(cx, name, win_src, wout_src, cw_src, cb_src, lng_src, lnb_src,
                  hT_src, hres_src, dst, dst_is_out, WIN, WOUT):
    S = cx.S
    m = cx.mark()
    cw = cx.f32(3 * NFT, "cw")
    cb = cx.f32(NFT, "cb")
    lng = cx.f32(D, "lng")
    lnb = cx.f32(D, "lnb")
    carry = cx.f32(2 * NFT, "carry")
    sm = [cx.f32(32, "sm%d" % i) for i in range(2)]
    hTin = [cx.bf16(NCT * TS, "hTin%d" % i) for i in range(2)]
    hres = [cx.f32(2 * D, "hres%d" % i) for i in range(2)]
    gated = [cx.bf16(TS, "gated%d" % j) for j in range(NPAIR)]
    ub = [cx.f32(TS + 2, "u%d" % i) for i in range(3)]
    eb = [cx.f32(TS, "e%d" % i) for i in range(4)]
    sg = [cx.f32(TS, "sg%d" % i) for i in range(2)]
    cx.dma("sp", cw.ap, cw_src.ap, [], [cw.b], name + "cw")
    cx.dma("sp", cb.ap, cb_src.ap, [], [cb.b], name + "cb")
    cx.dma("sp", lng.ap, lng_src.ap, [], [lng.b], name + "lng")
    cx.dma("sp", lnb.ap, lnb_src.ap, [], [lnb.b], name + "lnb")
    S.op("pool", lambda h, a=carry.ap: h.memset(a, 0.0), [], [carry.b])
    tiles = token_tiles()
    ucnt = 0
    for ti, (r0, n) in enumerate(tiles):
        subs = subtiles(n)
        hTi = hTin[ti % 2]
        hr = hres[ti % 2]
        cx.dma("sp", hTi.ap.rearrange("p (c t) -> p c t", c=NCT)[:, :, 0:n],
               hT_src.ap.rearrange("(c p) t -> p c t", p=128)[:, :, r0:r0 + n], [hT_src.b], [hTi.b], name + "hT%d" % (ti % 2))
        if n <= 128:
            cx.dma("sp", hr.ap[0:n, 0:D], hres_src.ap[r0:r0 + n, :], [hres_src.b], [hr.b], name + "hr%d" % (ti % 2))
        else:
            cx.dma("sp", hr.ap.rearrange("p (s d) -> p s d", s=2),
                   hres_src.ap[r0:r0 + n, :].rearrange("(s p) d -> p s d", p=128), [hres_src.b], [hr.b], name + "hr%d" % (ti % 2))
        for j in range(NPAIR):
            ccs = []
            for which, f in enumerate((j, j + NPAIR)):
                bank = ucnt % 2
                u = ub[ucnt % 3]
                e = eb[ucnt % 4]
                ucnt += 1
                ups = cx.ps[bank]
                for c in range(NCT):
                    cx.mm(ups[:, 0:n], WIN[c].ap[:, f * 128:(f + 1) * 128], hTi.ap[:, c * TS: c * TS + n],
                          c == 0, c == NCT - 1, [WIN[c].b, hTi.b], [cx.Bps[bank]])
                cx.copy("pool", u.ap[:, 0:2], carry.ap[:, 2 * f:2 * f + 2], [carry.b], [u.b])
                cx.act(u.ap[:, 2:2 + n], ups[:, 0:n], AF.Copy, [cx.Bps[bank]], [u.b])
                cx.act(e.ap[:, 0:n], ups[:, 0:n], AF.Identity, [cx.Bps[bank], cw.b, cb.b], [e.b],
                       bias=cb.ap[:, f:f + 1], scale=cw.ap[:, 2 * NFT + f: 2 * NFT + f + 1])
                cx.copy("pool", carry.ap[:, 2 * f:2 * f + 2], u.ap[:, n:n + 2], [u.b], [carry.b])
                cx.stt("dve", e.ap[:, 0:n], u.ap[:, 1:1 + n], cw.ap[:, NFT + f:NFT + f + 1], e.ap[:, 0:n],
                       ALU.mult, ALU.add, [u.b, e.b, cw.b], [e.b])
                cx.stt("dve", e.ap[:, 0:n], u.ap[:, 0:n], cw.ap[:, f:f + 1], e.ap[:, 0:n],
                       ALU.mult, ALU.add, [u.b, e.b, cw.b], [e.b])
                ccs.append(e)
            s_ = sg[j % 2]
            cx.act(s_.ap[:, 0:n], ccs[0].ap[:, 0:n], AF.Silu, [ccs[0].b], [s_.b])
            cx.tt("pool", gated[j].ap[:, 0:n], s_.ap[:, 0:n], ccs[1].ap[:, 0:n], ALU.mult, [s_.b, ccs[1].b], [gated[j].b])
        if ti == 0 and dst_is_out:
            continue
        for si, (o, ns) in enumerate(subs):
            for dh in range(2):
                bank = 2 + 2 * (si % 2) + dh
                for j in range(NPAIR):
                    cx.mm(cx.ps[bank][0:ns, :], gated[j].ap[:, o:o + ns], WOUT[j].ap[:, dh * 512:(dh + 1) * 512],
                          j == 0, j == NPAIR - 1, [gated[j].b, WOUT[j].b], [cx.Bps[bank]])
                cx.stt("dve", hr.ap[0:ns, si * D + dh * 512: si * D + (dh + 1) * 512],
                       hr.ap[0:ns, si * D + dh * 512: si * D + (dh + 1) * 512], ALPHA, cx.ps[bank][0:ns, :],
                       ALU.mult, ALU.add, [hr.b, cx.Bps[bank]], [hr.b])
        emit_ln(cx, hr, n, lng, lnb, sm)
        if dst_is_out:
            o0 = r0 - NMETA
            ev = cx.dma("sp", dst.ap[o0:o0 + n, :].rearrange("(s p) d -> p s d", p=128),
                        hr.ap.rearrange("p (s d) -> p s d", s=2), [hr.b], [dst.b], name + "st")
            cx.out_evs.append(ev)
        else:
            if n <= 128:
                ev = cx.dma("sp", dst.ap[r0:r0 + n, :], hr.ap[0:n, 0:D], [hr.b], [dst.b], name + "st")
            else:
                ev = cx.dma("sp", dst.ap[r0:r0 + n, :].rearrange("(s p) d -> p s d", p=128),
                            hr.ap.rearrange("p (s d) -> p s d", s=2), [hr.b], [dst.b], name + "st")
            cx.out_evs.append(ev)
    cx.release(m)


def load_ffn_weights(cx, WIN, WOUT, win_src, wout_src, name):
    wi = win_src.ap.rearrange("(c p) f -> p c f", p=128)
    for c in range(NCT):
        cx.dma("pool", WIN[c].ap, wi[:, c, :], [], [WIN[c].b], name + "wi%d" % c)
    wo = wout_src.ap.rearrange("(j p) d -> p j d", p=128)
    for j in range(NPAIR):
        cx.dma("pool", WOUT[j].ap, wo[:, j, :], [], [WOUT[j].b], name + "wo%d" % j)


def alloc_common(cx, dr_in):
    cx.ident = cx.bf16(128, "ident")
    cx.eps = cx.f32(2, "eps")
    ident_src = dr_in("ident", [128, 128], F32)
    cx.dma("pool", cx.ident.ap, ident_src, [], [cx.ident.b], "ident")
    cx.S.op("pool", lambda h, a=cx.eps.ap: h.memset(a, EPS), [], [cx.eps.b])


def finish(cx):
    lastd = {}
    for ev in cx.S.all_dma:
        lastd[ev.key] = ev
    cx.S.barrier("sp", list(lastd.values()))
    cx.S.emit(cx.st)


def build_A(upto=3, dbg=False):
    nc = bass.Bass("TRN2", target_bir_lowering=False)
    st = ExitStack()
    with st:
        def dr_in(name, shape, dt):
            return nc.dram_tensor(name, shape, dt, kind="ExternalInput").ap()

        def dr_out(name, shape, dt):
            return Tl(nc.dram_tensor(name, shape, dt, kind="ExternalOutput").ap(), name)

        def dr_int(name, shape, dt):
            return Tl(nc.dram_tensor(name, shape, dt, kind="Internal").ap(), name)

        cx = Cx(nc, st)
        S = cx.S
        xin = Tl(dr_in("xin", [NR, D], F32), "xin")
        bm_src = dr_in("band_main", [128, 512], F32)
        bh_src = dr_in("band_hist", [128, 512], F32)
        bp_src = dr_in("band_pro", [128, 64], F32)
        pw_src = dr_in("pool_wr", [128, 2048], F32)
        psc_src = dr_in("pool_sc", [128, D], F32)
        win_src = Tl(dr_in("w_in", [D, F2], F32))
        wout_src = Tl(dr_in("w_out", [F, D], F32))
        cw_src = Tl(dr_in("cw", [128, 3 * NFT], F32))
        cb_src = Tl(dr_in("cb", [128, NFT], F32))
        lng0 = Tl(dr_in("lng0", [128, D], F32))
        lnb0 = Tl(dr_in("lnb0", [128, D], F32))
        lng1 = Tl(dr_in("lng1", [128, D], F32))
        lnb1 = Tl(dr_in("lnb1", [128, D], F32))
        wkv_src = dr_in("w_kv", [D, 2 * D], F32)
        wq_src = dr_in("w_q", [D, D], F32)
        wf_src = dr_in("w_f", [D, H], F32)
        bf_src = dr_in("b_f", [H, 1], F32)
        h1 = dr_out("h1", [NTOK, D], F32)
        KT = dr_out("KT", [D, NTOK], BF16)
        QTo = dr_out("QT", [D, NTOK], BF16)
        Vo = dr_out("V", [NTOK, D], BF16)
        LF = dr_out("logfT", [H, NTOK], F32)
        h1a = (dr_out if dbg else dr_int)("h1a", [NTOK, D], F32)
        hT1 = (dr_out if dbg else dr_int)("hT1", [D, NTOK], BF16)

        alloc_common(cx, dr_in)
        mW0 = cx.mark()
        WIN = [cx.bf16(F2, "win%d" % c) for c in range(NCT)]
        WOUT = [cx.bf16(D, "wout%d" % j) for j in range(NPAIR)]
        if upto > 1:
            load_ffn_weights(cx, WIN, WOUT, win_src, wout_src, "A")

        m1 = cx.mark()
        bm = cx.bf16(512, "bm")
        bh = cx.bf16(512, "bh")
        bp = cx.bf16(64, "bp")
        Wp = cx.bf16(2048, "Wp")
        lng = cx.f32(D, "lng")
        lnb = cx.f32(D, "lnb")
        sm = [cx.f32(32, "sm%d" % i) for i in range(2)]
        xtm = [cx.f32(2 * D, "xtm%d" % i) for i in range(2)]
        xb = [cx.bf16(2 * D, "xb%d" % i) for i in range(2)]
        x128 = cx.f32(D, "x128")
        diffT = cx.bf16(NCT * TS, "diffT")
        hb = cx.bf16(2 * D, "hb")
        hT = [cx.bf16(NCT * TS, "hT%d" % i) for i in range(2)]
        cx.dma("pool", bm.ap, bm_src, [], [bm.b], "bm")
        cx.dma("pool", bh.ap, bh_src, [], [bh.b], "bh")
        cx.dma("pool", bp.ap, bp_src, [], [bp.b], "bp")
        cx.dma("sp", lng.ap, lng0.ap, [], [lng.b], "p1lng")
        cx.dma("sp", lnb.ap, lnb0.ap, [], [lnb.b], "p1lnb")
        Wpf = Tl(xtm[1].ap, "Wpf")
        Wpf.b = xtm[1].b
        psc = Tl(x128.ap, "psc")
        psc.b = x128.b
        cx.dma("sp", Wpf.ap, pw_src, [], [Wpf.b], "wpf")
        cx.dma("sp", psc.ap, psc_src, [], [psc.b], "psc")
        for g in range(4):
            for kc in range(2):
                o = (g * 2 + kc) * 256
                cx.tt("dve", Wp.ap[:, o:o + 256], Wpf.ap[:, o:o + 256], psc.ap[:, g * 256:(g + 1) * 256], ALU.mult,
                      [Wpf.b, psc.b], [Wp.b])

        tiles = token_tiles()
        for ti, (r0, n) in enumerate(tiles):
            subs = subtiles(n)
            xt = xtm[ti % 2]
            xbc = xb[ti % 2]
            row0 = PRE + r0
            if ti == 0:
                cx.dma("sp", x128.ap, xin.ap[0:128, :], [], [x128.b], "x128")
                cx.dma("sp", xt.ap[0:n, 0:D], xin.ap[row0:row0 + n, :], [], [xt.b], "xt%d" % (ti % 2))
                cx.copy("pool", xbc.ap[:, D:2 * D], x128.ap, [x128.b], [xbc.b])
            else:
                cx.dma("sp", xt.ap.rearrange("p (s d) -> p s d", s=2),
                       xin.ap[row0:row0 + n, :].rearrange("(s p) d -> p s d", p=128), [], [xt.b], "xt%d" % (ti % 2))
                cx.copy("pool", xbc.ap, xt.ap, [xt.b], [xbc.b])
            for c in range(NCT):
                g = c // 2
                bank = c % 2
                pb = cx.Bps[bank]
                if ti == 0:
                    cx.mm(cx.ps[bank][:, 0:n], xbc.ap[:, D + c * 128: D + (c + 1) * 128], bp.ap[:, g * 16:(g + 1) * 16],
                          True, True, [xbc.b, bp.b], [pb])
                else:
                    first = True
                    for si, (o, ns) in enumerate(subs):
                        if si == 0:
                            prev = xb[(ti - 1) % 2]
                            pap = prev.ap[64:128, D + c * 128: D + (c + 1) * 128]
                        else:
                            prev = xbc
                            pap = prev.ap[64:128, c * 128:(c + 1) * 128]
                        cx.mm(cx.ps[bank][:, o:o + ns], pap, bh.ap[64:128, g * 128:(g + 1) * 128],
                              first, False, [prev.b, bh.b], [pb], skip=True)
                        first = False
                        cx.mm(cx.ps[bank][:, o:o + ns], xbc.ap[:, si * D + c * 128: si * D + (c + 1) * 128],
                              bm.ap[:, g * 128:(g + 1) * 128], False, si == len(subs) - 1, [xbc.b, bm.b], [pb], skip=True)
                cx.act(diffT.ap[:, c * TS: c * TS + n], cx.ps[bank][:, 0:n], AF.Copy, [pb], [diffT.b])
            for si, (o, ns) in enumerate(subs):
                for g in range(4):
                    bank = 2 + 2 * (si % 2) + g // 2
                    col = (g % 2) * 256
                    for kc in range(2):
                        c = 2 * g + kc
                        cx.mm(cx.ps[bank][0:ns, col:col + 256], diffT.ap[:, c * TS + o: c * TS + o + ns],
                              Wp.ap[:, (g * 2 + kc) * 256:(g * 2 + kc + 1) * 256], kc == 0, kc == 1,
                              [diffT.b, Wp.b], [cx.Bps[bank]])
                for dh in range(2):
                    bank = 2 + 2 * (si % 2) + dh
                    xs = xt.ap[0:ns, si * D + dh * 512: si * D + (dh + 1) * 512]
                    cx.stt("dve", xs, xs, ALPHA, cx.ps[bank][0:ns, :], ALU.mult, ALU.add, [xt.b, cx.Bps[bank]], [xt.b])
            emit_ln(cx, xt, n, lng, lnb, sm)
            if n <= 128:
                cx.dma("sp", h1a.ap[r0:r0 + n, :], xt.ap[0:n, 0:D], [xt.b], [h1a.b], "h1ast")
            else:
                cx.dma("sp", h1a.ap[r0:r0 + n, :].rearrange("(s p) d -> p s d", p=128),
                       xt.ap.rearrange("p (s d) -> p s d", s=2), [xt.b], [h1a.b], "h1ast")
            emit_transposes(cx, xt, n, hb, hT[ti % 2], (hT1, r0), "hT1st%d" % (ti % 2))
        S.full_barrier()
        cx.release(m1)
        if upto == 1:
            finish(cx)
            return nc

        emit_ffn_pass(cx, "f0", win_src, wout_src, cw_src, cb_src, lng1, lnb1, hT1, h1a, h1, False, WIN, WOUT)
        S.full_barrier()
        cx.release(mW0)
        if upto == 2:
            finish(cx)
            return nc
        emit_proj_pass(cx, wkv_src, wq_src, wf_src, bf_src, h1, KT, QTo, Vo, LF)
        finish(cx)
    return nc


def emit_proj_pass(cx, wkv_src, wq_src, wf_src, bf_src, h1, KT, QTo, Vo, LF):
    S = cx.S
    WK = [cx.bf16(D, "wk%d" % c) for c in range(NCT)]
    WV = [cx.bf16(D, "wv%d" % c) for c in range(NCT)]
    WQ = [cx.bf16(D, "wq%d" % c) for c in range(NCT)]
    WF = cx.bf16(NCT * H, "wf")
    nbf = cx.f32(2, "nbf")
    wk = wkv_src.rearrange("(c p) f -> p c f", p=128)
    wq = wq_src.rearrange("(c p) f -> p c f", p=128)
    for c in range(NCT):
        cx.dma("pool", WK[c].ap, wk[:, c, 0:D], [], [WK[c].b], "wk%d" % c)
        cx.dma("pool", WQ[c].ap, wq[:, c, :], [], [WQ[c].b], "wq%d" % c)
        cx.dma("pool", WV[c].ap, wk[:, c, D:2 * D], [], [WV[c].b], "wv%d" % c)
    cx.dma("pool", WF.ap.rearrange("p (c h) -> p c h", c=NCT), wf_src.rearrange("(c p) h -> p c h", p=128), [], [WF.b], "wf")
    cx.dma("sp", nbf.ap[0:H, 0:1], bf_src, [], [nbf.b], "bf")
    cx.ts("dve", nbf.ap[0:H, 0:1], nbf.ap[0:H, 0:1], -1.0, None, ALU.mult, None, [nbf.b], [nbf.b])
    hin = [cx.f32(2 * D, "hin%d" % i) for i in range(2)]
    hb = cx.bf16(2 * D, "hb")
    hT = [cx.bf16(NCT * TS, "hT%d" % i) for i in range(2)]
    kst = [cx.bf16(NCT * TS, "kst%d" % i) for i in range(2)]
    qst = [cx.bf16(NCT * TS, "qst%d" % i) for i in range(2)]
    vst = [cx.bf16(2 * D, "vst%d" % i) for i in range(2)]
    lst = [cx.f32(TS, "lst%d" % i) for i in range(2)]
    tiles = token_tiles()
    cnt = 0
    for ti, (r0, n) in enumerate(tiles):
        subs = subtiles(n)
        hi = hin[ti % 2]
        hTt = hT[ti % 2]
        if n <= 128:
            cx.dma("sp", hi.ap[0:n, 0:D], h1.ap[r0:r0 + n, :], [h1.b], [hi.b], "hin%d" % (ti % 2))
        else:
            cx.dma("sp", hi.ap.rearrange("p (s d) -> p s d", s=2),
                   h1.ap[r0:r0 + n, :].rearrange("(s p) d -> p s d", p=128), [h1.b], [hi.b], "hin%d" % (ti % 2))
        emit_transposes(cx, hi, n, hb, hTt)
        ks, qs, vs, ls = kst[ti % 2], qst[ti % 2], vst[ti % 2], lst[ti % 2]
        for (W, stg, scale) in ((WK, ks, 1.0), (WQ, qs, 0.125)):
            for kd in range(NCT):
                bank = cnt % 2
                cnt += 1
                for c in range(NCT):
                    cx.mm(cx.ps[bank][:, 0:n], W[c].ap[:, kd * 128:(kd + 1) * 128], hTt.ap[:, c * TS:c * TS + n],
                          c == 0, c == NCT - 1, [W[c].b, hTt.b], [cx.Bps[bank]])
                cx.act(stg.ap[:, kd * TS: kd * TS + n], cx.ps[bank][:, 0:n], AF.Identity, [cx.Bps[bank]], [stg.b], scale=scale)
        cx.dma("sp", KT.ap.rearrange("(c p) t -> p c t", p=128)[:, :, r0:r0 + n],
               ks.ap.rearrange("p (c t) -> p c t", c=NCT)[:, :, 0:n], [ks.b], [KT.b], "kst%d" % (ti % 2))
        cx.out_evs.append(cx.dma("sp", QTo.ap.rearrange("(c p) t -> p c t", p=128)[:, :, r0:r0 + n],
                                 qs.ap.rearrange("p (c t) -> p c t", c=NCT)[:, :, 0:n], [qs.b], [QTo.b], "qst%d" % (ti % 2)))
        for si, (o, ns) in enumerate(subs):
            for dh in range(2):
                bank = 2 + 2 * (si % 2) + dh
                for c in range(NCT):
                    cx.mm(cx.ps[bank][0:ns, :], hTt.ap[:, c * TS + o: c * TS + o + ns], WV[c].ap[:, dh * 512:(dh + 1) * 512],
                          c == 0, c == NCT - 1, [hTt.b, WV[c].b], [cx.Bps[bank]])
                cx.copy("dve", vs.ap[0:ns, si * D + dh * 512: si * D + (dh + 1) * 512], cx.ps[bank][0:ns, :], [cx.Bps[bank]], [vs.b])
        if n <= 128:
            ev = cx.dma("sp", Vo.ap[r0:r0 + n, :], vs.ap[0:n, 0:D], [vs.b], [Vo.b], "vst%d" % (ti % 2))
        else:
            ev = cx.dma("sp", Vo.ap[r0:r0 + n, :].rearrange("(s p) d -> p s d", p=128),
                        vs.ap.rearrange("p (s d) -> p s d", s=2), [vs.b], [Vo.b], "vst%d" % (ti % 2))
        cx.out_evs.append(ev)
        bank = cnt % 2
        cnt += 1
        for c in range(NCT):
            cx.mm(cx.ps[bank][0:H, 0:n], WF.ap[:, c * H:(c + 1) * H], hTt.ap[:, c * TS:c * TS + n],
                  c == 0, c == NCT - 1, [WF.b, hTt.b], [cx.Bps[bank]])
        cx.act(ls.ap[0:H, 0:n], cx.ps[bank][0:H, 0:n], AF.Exp, [cx.Bps[bank], nbf.b], [ls.b], bias=nbf.ap[0:H, 0:1], scale=-1.0)
        cx.act(ls.ap[0:H, 0:n], ls.ap[0:H, 0:n], AF.Ln, [ls.b], [ls.b], bias=1.0, scale=1.0)
        cx.ts("dve", ls.ap[0:H, 0:n], ls.ap[0:H, 0:n], -1.0, None, ALU.mult, None, [ls.b], [ls.b])
        cx.out_evs.append(cx.dma("sp", LF.ap[:, r0:r0 + n], ls.ap[0:H, 0:n], [ls.b], [LF.b], "lst%d" % (ti % 2)))


def _bands(r):
    W = (2, 4, 8, 16)
    bm = np.zeros((128, 4, 128), np.float32)
    bh = np.zeros((128, 4, 128), np.float32)
    bp = np.zeros((128, 4, 16), np.float32)
    s = np.arange(128)[:, None]
    t = np.arange(128)[None, :]
    tt = np.arange(16)[None, :]
    pos = PRE + tt
    for g, w in enumerate(W):
        d = t - s
        bm[:, g, :] = ((d >= 0) & (d < w)) / float(w) - (d == 0)
        d2 = t - (s - 128)
        bh[:, g, :] = ((d2 >= 0) & (d2 < w)) / float(w)
        cnt = np.minimum(tt + 1, w).astype(np.float32) if r == 0 else np.full((1, 16), float(w), np.float32)
        d3 = pos - s
        bp[:, g, :] = ((d3 >= 0) & (d3 < w)) / cnt - (d3 == 0)
    bh[:64] = 0.0
    return bm.reshape(128, 512), bh.reshape(128, 512), bp.reshape(128, 64)


def _rep(v):
    return np.ascontiguousarray(np.broadcast_to(np.asarray(v, np.float32)[None, :], (128, v.shape[0])))


def _xin(x, meta, b, r):
    if r == 0:
        return np.ascontiguousarray(np.concatenate([np.zeros((PRE, D), np.float32), meta, x[b, 0:NT * TS]], 0))
    lo = NT * TS - NMETA - PRE
    return np.ascontiguousarray(x[b, lo:SEQ])


def _cw(conv_w_l):
    return np.ascontiguousarray(conv_w_l.reshape(3, NFT, 128).transpose(2, 0, 1).reshape(128, 3 * NFT))


def _cb(conv_b_l):
    return np.ascontiguousarray(conv_b_l.reshape(NFT, 128).T)


def in_maps_A(inp):
    maps = []
    pw = np.ascontiguousarray(inp["pool_w"][0].reshape(4, 2, 128, 256).transpose(2, 0, 1, 3).reshape(128, 2048))
    ident = np.eye(128, dtype=np.float32)
    for core in range(8):
        b, r = core // 2, core % 2
        bm, bh, bp = _bands(r)
        maps.append({
            "xin": _xin(inp["x"], inp["meta"], b, r),
            "band_main": bm, "band_hist": bh, "band_pro": bp, "ident": ident,
            "pool_wr": pw, "pool_sc": _rep(inp["pool_scale"][0]),
            "w_in": inp["ffn_w_in"][0], "w_out": inp["ffn_w_out"][0],
            "cw": _cw(inp["ffn_conv_w"][0]), "cb": _cb(inp["ffn_conv_b"][0]),
            "lng0": _rep(inp["ln_g"][0, 0]), "lnb0": _rep(inp["ln_b"][0, 0]),
            "lng1": _rep(inp["ln_g"][0, 1]), "lnb1": _rep(inp["ln_b"][0, 1]),
            "w_kv": inp["w_kv"], "w_q": inp["w_q"][0], "w_f": inp["w_f"],
            "b_f": np.ascontiguousarray(inp["b_f"][:, None]),
        })
    return maps


def attn_steps(heads):
    steps = []
    for hh in heads:
        qts = [(0, NMETA, True)] + [(NMETA + QT_ * X, QT_, False) for X in range(SEQ // QT_)]
        for qi, (q0, nq, is_meta) in enumerate(qts):
            kbs = []
            if is_meta:
                kbs.append((0, NMETA, True, 0))
            else:
                kbs.append((0, NMETA, False, 0))
                nfull = (q0 - NMETA) // 128
                for b in range(nfull):
                    kbs.append((NMETA + 128 * b, 128, False, 0))
                for j in range(nq // 128):
                    kbs.append((q0 + 128 * j, 128, True, 128 * j))
            for ki, kb in enumerate(kbs):
                steps.append((hh, qi, q0, nq, kb, ki == 0, ki == len(kbs) - 1))
    return steps


def emit_attention(cx, KTs, QTs, Vs, KX, QX, OT, heads, hbase=0):
    S = cx.S
    NSB = 4
    kaug = [cx.bf16(T, "kaug%d" % i) for i in range(2)]
    qaug = [cx.bf16(T, "qaug%d" % i) for i in range(2)]
    nblk = (T - NMETA) // 128
    vaug = [cx.bf16((1 + nblk) * 65, "vaug%d" % i) for i in range(2)]
    Pb = [cx.bf16(QT_, "P%d" % i) for i in range(NSB)]
    rec = cx.f32(QT_, "rec")
    rhi = cx.bf16(QT_, "rhi")
    rlo = cx.bf16(QT_, "rlo")
    bcs = cx.f32(QT_, "bcs")
    ot = [cx.bf16(QT_, "ot%d" % i) for i in range(2)]
    for i in range(2):
        S.op("pool", lambda h, a=vaug[i].ap: h.memset(a, 1.0), [], [vaug[i].b])

    def load_head(idx, hh):
        s = idx % 2
        cx.dma("sp", kaug[s].ap[0:DH, :], KTs.ap[hh * DH:(hh + 1) * DH, :], [KTs.b], [kaug[s].b], "ka%d" % s)
        cx.dma("sp", kaug[s].ap[DH:DH + 6, :], KX.ap[hh], [KX.b, kaug[s].b], [kaug[s].b], "kx%d" % s)
        cx.dma("sp", qaug[s].ap[0:DH, :], QTs.ap[hh * DH:(hh + 1) * DH, :], [QTs.b], [qaug[s].b], "qa%d" % s)
        cx.dma("sp", qaug[s].ap[DH:DH + 6, :], QX.ap[hh], [QX.b, qaug[s].b], [qaug[s].b], "qx%d" % s)
        cx.dma("act", vaug[s].ap[0:NMETA, 0:DH], Vs.ap[0:NMETA, hh * DH:(hh + 1) * DH], [Vs.b], [vaug[s].b], "va%d" % s)
        cx.dma("act", vaug[s].ap[:, 65:(1 + nblk) * 65].rearrange("p (b e) -> p b e", e=65)[:, :, 0:DH],
               Vs.ap[NMETA:T, hh * DH:(hh + 1) * DH].rearrange("(b p) d -> p b d", p=128),
               [Vs.b, vaug[s].b], [vaug[s].b], "vb%d" % s)

    steps = attn_steps(heads)
    hidx = {hh: i for i, hh in enumerate(heads)}
    load_head(0, heads[0])
    loaded = 1
    LA = 2
    deferred = []
    qcount = [0]

    def emit_S(i):
        hh, qi, q0, nq, (k0, nk, diag, cq0), first, last = steps[i]
        s = hidx[hh] % 2
        bank = i % NSB
        ps, pb = cx.ps[bank], cx.Bps[bank]
        ka, qa = kaug[s], qaug[s]
        if not diag:
            cx.mm(ps[0:nk, 0:nq], ka.ap[0:DH + 6, k0:k0 + nk], qa.ap[0:DH + 6, q0:q0 + nq], True, True, [ka.b, qa.b], [pb])
        else:
            cx.mm(ps[0:nk, cq0:cq0 + nk], ka.ap[0:DH + 6, k0:k0 + nk], qa.ap[0:DH + 6, q0 + cq0:q0 + cq0 + nk],
                  True, False, [ka.b, qa.b], [pb], skip=True)
            cx.mm(ps[0:nk, cq0:cq0 + nk], cx.ident.ap[0:nk, 0:nk], cx.maskb.ap[0:nk, 0:nk], False, True,
                  [cx.ident.b, cx.maskb.b], [pb], skip=True)
            if cq0 + nk < nq:
                cx.mm(ps[0:nk, cq0 + nk:nq], ka.ap[0:DH + 6, k0:k0 + nk], qa.ap[0:DH + 6, q0 + cq0 + nk:q0 + nq],
                      True, True, [ka.b, qa.b], [pb], skip=True)

    def emit_rest(i):
        hh, qi, q0, nq, (k0, nk, diag, cq0), first, last = steps[i]
        s = hidx[hh] % 2
        bank = i % NSB
        ps, pb = cx.ps[bank], cx.Bps[bank]
        P = Pb[bank]
        if first:
            qcount[0] += 1
            while len(deferred) > 1:
                deferred.pop(0)[1]()
        ob = 4 + (qcount[0] % 2)
        ops_, opb = cx.ps[ob], cx.Bps[ob]
        cx.act(P.ap[0:nk, cq0:nq], ps[0:nk, cq0:nq], AF.Exp, [pb], [P.b])
        blk = 0 if k0 == 0 else 1 + (k0 - NMETA) // 128
        va = vaug[s]
        cx.mm(ops_[0:DH + 1, cq0:nq], va.ap[0:nk, blk * 65:(blk + 1) * 65], P.ap[0:nk, cq0:nq], first, last,
              [va.b, P.b], [opb], skip=True)
        if last:
            par = qcount[0] % 2
            o_t = ot[par]

            def fin(hh=hh, q0=q0, nq=nq, ops_=ops_, opb=opb, o_t=o_t, par=par):
                cx.S.op("dve", lambda h, a=rec.ap[DH:DH + 1, 0:nq], b=ops_[DH:DH + 1, 0:nq]: h.reciprocal(a, b), [opb], [rec.b])
                cx.copy("dve", rhi.ap[DH:DH + 1, 0:nq], rec.ap[DH:DH + 1, 0:nq], [rec.b], [rhi.b])
                cx.tt("dve", rlo.ap[DH:DH + 1, 0:nq], rec.ap[DH:DH + 1, 0:nq], rhi.ap[DH:DH + 1, 0:nq], ALU.subtract,
                      [rec.b, rhi.b], [rlo.b])
                bps, bpb = cx.ps[6], cx.Bps[6]
                cx.mm(bps[0:DH, 0:nq], cx.ones.ap[DH:DH + 1, 0:DH], rhi.ap[DH:DH + 1, 0:nq], True, False, [cx.ones.b, rhi.b], [bpb])
                cx.mm(bps[0:DH, 0:nq], cx.ones.ap[DH:DH + 1, 0:DH], rlo.ap[DH:DH + 1, 0:nq], False, True, [cx.ones.b, rlo.b], [bpb])
                cx.copy("dve", bcs.ap[0:DH, 0:nq], bps[0:DH, 0:nq], [bpb], [bcs.b])
                cx.tt("dve", o_t.ap[0:DH, 0:nq], ops_[0:DH, 0:nq], bcs.ap[0:DH, 0:nq], ALU.mult, [opb, bcs.b], [o_t.b])
                hl = hh - hbase
                cx.dma("sp", OT.ap[hl * DH:(hl + 1) * DH, q0:q0 + nq], o_t.ap[0:DH, 0:nq], [o_t.b], [OT.b], "ot%d" % par)
            deferred.append((i + 3, fin))

    N = len(steps)
    for i in range(N + LA):
        if i < N:
            emit_S(i)
        if i >= LA:
            j = i - LA
            emit_rest(j)
            if (j == 0 or steps[j - 1][0] != steps[j][0]) and loaded < len(heads) and hidx[steps[j][0]] + 1 == loaded:
                load_head(loaded, heads[loaded])
                loaded += 1
        while deferred and deferred[0][0] <= i - LA:
            deferred.pop(0)[1]()
    while deferred:
        deferred.pop(0)[1]()


def emit_cprep(cx, LFs, KX, QX, nh):
    S = cx.S
    m = cx.mark()
    nchunk = 4
    CH = T // nchunk
    lf = cx.f32(CH, "lf")
    onesf = cx.f32(CH, "onesf")
    cc = [cx.f32(CH, "cc%d" % i) for i in range(2)]
    r1 = cx.f32(CH, "r1")
    hi = cx.bf16(CH, "hi")
    mid = cx.bf16(CH, "mid")
    lo = cx.bf16(CH, "lo")
    nhi = cx.bf16(CH, "nhi")
    nmid = cx.bf16(CH, "nmid")
    nlo = cx.bf16(CH, "nlo")
    oneb = cx.bf16(CH, "oneb")
    S.op("pool", lambda h, a=onesf.ap: h.memset(a, 1.0), [], [onesf.b])
    S.op("pool", lambda h, a=oneb.ap: h.memset(a, 1.0), [], [oneb.b])
    P = slice(0, nh)
    for ci in range(nchunk):
        cs = slice(ci * CH, (ci + 1) * CH)
        c_ = cc[ci % 2]
        cx.dma("sp", lf.ap[P, :], LFs.ap[:, cs], [LFs.b], [lf.b], "lf")
        if ci == 0:
            S.op("dve", lambda h, o=c_.ap[P, :], a=onesf.ap[P, :], b=lf.ap[P, :]:
                 h.tensor_tensor_scan(o, a, b, 0.0, op0=ALU.mult, op1=ALU.add), [onesf.b, lf.b], [c_.b])
        else:
            pv = cc[(ci - 1) % 2]
            S.op("dve", lambda h, o=c_.ap[P, :], a=onesf.ap[P, :], b=lf.ap[P, :], i=pv.ap[P, CH - 1:CH]:
                 h.tensor_tensor_scan(o, a, b, i, op0=ALU.mult, op1=ALU.add), [onesf.b, lf.b, pv.b], [c_.b])
        cx.copy("dve", hi.ap[P, :], c_.ap[P, :], [c_.b], [hi.b])
        cx.tt("dve", r1.ap[P, :], c_.ap[P, :], hi.ap[P, :], ALU.subtract, [c_.b, hi.b], [r1.b])
        cx.copy("dve", mid.ap[P, :], r1.ap[P, :], [r1.b], [mid.b])
        cx.tt("dve", r1.ap[P, :], r1.ap[P, :], mid.ap[P, :], ALU.subtract, [r1.b, mid.b], [r1.b])
        cx.copy("dve", lo.ap[P, :], r1.ap[P, :], [r1.b], [lo.b])
        for (src, dst) in ((hi, nhi), (mid, nmid), (lo, nlo)):
            cx.ts("pool", dst.ap[P, :], src.ap[P, :], -1.0, None, ALU.mult, None, [src.b], [dst.b])
        for j, tl in enumerate((oneb, oneb, oneb, nhi, nmid, nlo)):
            cx.dma("sp", KX.ap[:, j, cs], tl.ap[P, :], [tl.b], [KX.b], "kxw")
        for j, tl in enumerate((hi, mid, lo, oneb, oneb, oneb)):
            cx.dma("sp", QX.ap[:, j, cs], tl.ap[P, :], [tl.b], [QX.b], "qxw")
    cx.S.full_barrier()
    cx.release(m)


def alloc_attn_consts(cx, dr_in):
    cx.maskb = cx.bf16(128, "maskb")
    cx.ones = cx.bf16(DH, "ones")
    mask_src = dr_in("maskT", [128, 128], F32)
    cx.dma("pool", cx.maskb.ap, mask_src, [], [cx.maskb.b], "mask")
    cx.S.op("pool", lambda h, a=cx.ones.ap: h.memset(a, 1.0), [], [cx.ones.b])


def build_B():
    nc = bass.Bass("TRN2", target_bir_lowering=False)
    st = ExitStack()
    with st:
        def dr_in(name, shape, dt):
            return nc.dram_tensor(name, shape, dt, kind="ExternalInput").ap()

        cx = Cx(nc, st)
        KTs = Tl(dr_in("KTh", [NH * DH, T], BF16))
        QTs = Tl(dr_in("QTh", [NH * DH, T], BF16))
        Vs = Tl(dr_in("Vh", [T, NH * DH], BF16))
        LFs = Tl(dr_in("LFh", [NH, T], F32))
        OT = Tl(nc.dram_tensor("OTh", [NH * DH, T], BF16, kind="ExternalOutput").ap())
        KX = Tl(nc.dram_tensor("KX", [NH, 6, T], BF16, kind="Internal").ap())
        QX = Tl(nc.dram_tensor("QX", [NH, 6, T], BF16, kind="Internal").ap())
        alloc_common(cx, dr_in)
        alloc_attn_consts(cx, dr_in)
        emit_cprep(cx, LFs, KX, QX, NH)
        emit_attention(cx, KTs, QTs, Vs, KX, QX, OT, list(range(NH)))
        finish(cx)
    return nc


def emit_attnproj_pass(cx, wo_src, lng_src, lnb_src, OTl, h1, h2a, hT2):
    S = cx.S
    m = cx.mark()
    WO = [cx.bf16(D, "wo%d" % c) for c in range(NCT)]
    wo = wo_src.rearrange("(c p) f -> p c f", p=128)
    for c in range(NCT):
        cx.dma("pool", WO[c].ap, wo[:, c, :], [], [WO[c].b], "wo%d" % c)
    lng = cx.f32(D, "lng")
    lnb = cx.f32(D, "lnb")
    cx.dma("sp", lng.ap, lng_src.ap, [], [lng.b], "p4lng")
    cx.dma("sp", lnb.ap, lnb_src.ap, [], [lnb.b], "p4lnb")
    sm = [cx.f32(32, "sm%d" % i) for i in range(2)]
    oin = [cx.bf16(NCT * TS, "oin%d" % i) for i in range(2)]
    hin = [cx.f32(2 * D, "hin%d" % i) for i in range(2)]
    hb = cx.bf16(2 * D, "hb")
    hT = [cx.bf16(NCT * TS, "hT%d" % i) for i in range(2)]
    for ti, (r0, n) in enumerate(token_tiles()):
        subs = subtiles(n)
        oi, hi = oin[ti % 2], hin[ti % 2]
        cx.dma("sp", oi.ap.rearrange("p (c t) -> p c t", c=NCT)[:, :, 0:n],
               OTl.ap.rearrange("(c p) t -> p c t", p=128)[:, :, r0:r0 + n], [OTl.b], [oi.b], "oin%d" % (ti % 2))
        if n <= 128:
            cx.dma("sp", hi.ap[0:n, 0:D], h1.ap[r0:r0 + n, :], [h1.b], [hi.b], "p4h%d" % (ti % 2))
        else:
            cx.dma("sp", hi.ap.rearrange("p (s d) -> p s d", s=2),
                   h1.ap[r0:r0 + n, :].rearrange("(s p) d -> p s d", p=128), [h1.b], [hi.b], "p4h%d" % (ti % 2))
        for si, (o, ns) in enumerate(subs):
            for dh in range(2):
                bank = 2 + 2 * (si % 2) + dh
                for c in range(NCT):
                    cx.mm(cx.ps[bank][0:ns, :], oi.ap[:, c * TS + o: c * TS + o + ns], WO[c].ap[:, dh * 512:(dh + 1) * 512],
                          c == 0, c == NCT - 1, [oi.b, WO[c].b], [cx.Bps[bank]])
                sl = hi.ap[0:ns, si * D + dh * 512: si * D + (dh + 1) * 512]
                cx.stt("dve", sl, sl, ALPHA, cx.ps[bank][0:ns, :], ALU.mult, ALU.add, [hi.b, cx.Bps[bank]], [hi.b])
        emit_ln(cx, hi, n, lng, lnb, sm)
        if n <= 128:
            cx.dma("sp", h2a.ap[r0:r0 + n, :], hi.ap[0:n, 0:D], [hi.b], [h2a.b], "h2ast")
        else:
            cx.dma("sp", h2a.ap[r0:r0 + n, :].rearrange("(s p) d -> p s d", p=128),
                   hi.ap.rearrange("p (s d) -> p s d", s=2), [hi.b], [h2a.b], "h2ast")
        emit_transposes(cx, hi, n, hb, hT[ti % 2], (hT2, r0), "hT2st%d" % (ti % 2))
    S.full_barrier()
    cx.release(m)


def build_C():
    nc = bass.Bass("TRN2", target_bir_lowering=False)
    st = ExitStack()
    with st:
        def dr_in(name, shape, dt):
            return nc.dram_tensor(name, shape, dt, kind="ExternalInput").ap()

        cx = Cx(nc, st)
        OTl = Tl(dr_in("OTl", [D, NTOK], BF16))
        h1 = Tl(dr_in("h1", [NTOK, D], F32))
        wo_src = dr_in("w_o", [D, D], F32)
        win_src = Tl(dr_in("w_in", [D, F2], F32))
        wout_src = Tl(dr_in("w_out", [F, D], F32))
        cw_src = Tl(dr_in("cw", [128, 3 * NFT], F32))
        cb_src = Tl(dr_in("cb", [128, NFT], F32))
        lng0 = Tl(dr_in("lng0", [128, D], F32))
        lnb0 = Tl(dr_in("lnb0", [128, D], F32))
        lng1 = Tl(dr_in("lng1", [128, D], F32))
        lnb1 = Tl(dr_in("lnb1", [128, D], F32))
        out = Tl(nc.dram_tensor("out", [NT * TS, D], F32, kind="ExternalOutput").ap())
        h2a = Tl(nc.dram_tensor("h2a", [NTOK, D], F32, kind="Internal").ap())
        hT2 = Tl(nc.dram_tensor("hT2", [D, NTOK], BF16, kind="Internal").ap())
        alloc_common(cx, dr_in)
        WIN = [cx.bf16(F2, "win%d" % c) for c in range(NCT)]
        WOUT = [cx.bf16(D, "wout%d" % j) for j in range(NPAIR)]
        load_ffn_weights(cx, WIN, WOUT, win_src, wout_src, "C")
        emit_attnproj_pass(cx, wo_src, lng0, lnb0, OTl, h1, h2a, hT2)
        emit_ffn_pass(cx, "f1", win_src, wout_src, cw_src, cb_src, lng1, lnb1, hT2, h2a, out, True, WIN, WOUT)
        finish(cx)
    return nc


def _maskT():
    p = np.arange(128)[:, None]
    c = np.arange(128)[None, :]
    return np.where(p <= c, 0.0, NEG).astype(np.float32)


def in_maps_C_weights(inp):
    return {
        "ident": np.eye(128, dtype=np.float32),
        "w_o": inp["w_o"][0], "w_in": inp["ffn_w_in"][1], "w_out": inp["ffn_w_out"][1],
        "cw": _cw(inp["ffn_conv_w"][1]), "cb": _cb(inp["ffn_conv_b"][1]),
        "lng0": _rep(inp["ln_g"][1, 0]), "lnb0": _rep(inp["ln_b"][1, 0]),
        "lng1": _rep(inp["ln_g"][1, 1]), "lnb1": _rep(inp["ln_b"][1, 1]),
    }


_PROGS = {}


def _prog(name, fn):
    if name not in _PROGS:
        _PROGS[name] = fn()
    return _PROGS[name]


def kernel(x, meta, pool_w, pool_scale, w_kv, w_f, b_f, w_q, w_o, ffn_w_in, ffn_conv_w, ffn_conv_b,
           ffn_w_out, ln_g, ln_b):
    inp = {k: np.asarray(v, np.float32) for k, v in dict(
        x=x, meta=meta, pool_w=pool_w, pool_scale=pool_scale, w_kv=w_kv, w_f=w_f, b_f=b_f, w_q=w_q, w_o=w_o,
        ffn_w_in=ffn_w_in, ffn_conv_w=ffn_conv_w, ffn_conv_b=ffn_conv_b, ffn_w_out=ffn_w_out,
        ln_g=ln_g, ln_b=ln_b).items()}
    cores = list(range(8))
    resA = run_bass_kernel_spmd(_prog("A", build_A), in_maps_A(inp), core_ids=cores).results
    ident = np.eye(128, dtype=np.float32)
    mask = _maskT()
    mapsB = []
    for core in cores:
        b, r = core // 2, core % 2
        A0, A1 = resA[2 * b], resA[2 * b + 1]
        hs = slice(r * NH * DH, (r + 1) * NH * DH)
        mapsB.append({
            "KTh": np.ascontiguousarray(np.concatenate([A0["KT"][hs], A1["KT"][hs][:, NMETA:]], 1)),
            "QTh": np.ascontiguousarray(np.concatenate([A0["QT"][hs], A1["QT"][hs][:, NMETA:]], 1)),
            "Vh": np.ascontiguousarray(np.concatenate([A0["V"][:, hs], A1["V"][NMETA:, hs]], 0)),
            "LFh": np.ascontiguousarray(np.concatenate([A0["logfT"][r * NH:(r + 1) * NH],
                                                        A1["logfT"][r * NH:(r + 1) * NH, NMETA:]], 1)),
            "maskT": mask, "ident": ident,
        })
    resB = run_bass_kernel_spmd(_prog("B", build_B), mapsB, core_ids=cores).results
    wC = in_maps_C_weights(inp)
    mapsC = []
    for core in cores:
        b, r = core // 2, core % 2
        lo = r * NT * TS
        m = dict(wC)
        m["OTl"] = np.ascontiguousarray(np.concatenate([resB[2 * b]["OTh"][:, lo:lo + NTOK],
                                                        resB[2 * b + 1]["OTh"][:, lo:lo + NTOK]], 0))
        m["h1"] = resA[core]["h1"]
        mapsC.append(m)
    resC = run_bass_kernel_spmd(_prog("C", build_C), mapsC, core_ids=cores).results
    out = np.empty((BATCH, SEQ, D), np.float32)
    for core in cores:
        b, r = core // 2, core % 2
        out[b, r * NT * TS:(r + 1) * NT * TS] = resC[core]["out"]
    return out
```
